# Optimizing a Trainium2 kernel written in Bass

```python
import math
import jax, jax.numpy as jnp
from jax import lax
import numpy as np

D_MODEL = 2048
BATCH = 4
SEQ = 4096
DEPTH = 2
DEC_BATCH = 2
DEC_SEQ = 4096
PAST_LEN = 128

MIX_WIDTH = D_MODEL
A_HEADS = 6
A_HEAD_DIM = 128
A_WIDTH = A_HEADS * A_HEAD_DIM
CHUNK = 128
B_HEADS = 4
B_HEAD_DIM = 128
B_WIDTH = B_HEADS * B_HEAD_DIM
C_WIDTH = MIX_WIDTH - A_WIDTH - B_WIDTH
C_GROUP = 16
C_GROUPS = C_WIDTH // C_GROUP
C_STATE = 64
IN_WIDTH = 2 * A_WIDTH + B_WIDTH + C_WIDTH
D_FF = -(-8 * D_MODEL // (3 * 256)) * 256
EPS = 1e-6

kernel_name = "hybrid_gmlp_fnet_s5_encoder"


def rms_norm(x, g):
    xf = x.astype(jnp.float32)
    y = xf * lax.rsqrt(jnp.mean(xf * xf, axis=-1, keepdims=True) + EPS)
    return (y * g.astype(jnp.float32)).astype(x.dtype)


def gmlp_mixer(z, v_gain, w_s, b_s):
    nb, s, _ = z.shape
    z = jax.nn.gelu(z)
    u, v = jnp.split(z, 2, axis=-1)
    v = rms_norm(v.reshape(nb, s, A_HEADS, A_HEAD_DIM), v_gain.reshape(A_HEADS, A_HEAD_DIM))
    vc = v.reshape(nb, s // CHUNK, CHUNK, A_HEADS, A_HEAD_DIM)
    mixed = jnp.einsum('hqk,bnkhd->bnqhd', w_s, vc) + b_s.T[None, None, :, :, None]
    return u * mixed.reshape(nb, s, A_WIDTH)


def fourier_mixer(z):
    nb, s, _ = z.shape
    zh = z.reshape(nb, s, B_HEADS, B_HEAD_DIM).astype(jnp.float32)
    f = jnp.fft.fft2(zh, axes=(1, 3), norm='ortho').real
    return f.reshape(nb, s, B_WIDTH).astype(z.dtype)


def _complex_diag_scan(a_re, a_im, x_re, x_im, reverse):
    def combine(e1, e2):
        a1r, a1i, x1r, x1i = e1
        a2r, a2i, x2r, x2i = e2
        return (a2r * a1r - a2i * a1i,
                a2r * a1i + a2i * a1r,
                a2r * x1r - a2i * x1i + x2r,
                a2r * x1i + a2i * x1r + x2i)
    ar = jnp.broadcast_to(a_re, x_re.shape)
    ai = jnp.broadcast_to(a_im, x_re.shape)
    _, _, hr, hi = lax.associative_scan(combine, (ar, ai, x_re, x_im), axis=1, reverse=reverse)
    return hr, hi


def s5_mixer(z, lam_re, lam_im, log_step, b_re, b_im, c_re, c_im, d_skip, w_glu, b_glu):
    nb, s, _ = z.shape
    f32 = jnp.float32
    u = z.reshape(nb, s, C_GROUPS, C_GROUP).astype(f32)
    lr = lam_re.astype(f32)
    li = lam_im.astype(f32)
    step = jnp.exp(log_step.astype(f32))[..., None]
    mag = jnp.exp(lr * step)
    ab_re = mag * jnp.cos(li * step)
    ab_im = mag * jnp.sin(li * step)
    den = lr * lr + li * li
    nr = ab_re - 1.0
    q_re = (nr * lr + ab_im * li) / den
    q_im = (ab_im * lr - nr * li) / den
    br = b_re.astype(f32)[None]
    bi = b_im.astype(f32)[None]
    bb_re = q_re[..., None] * br - q_im[..., None] * bi
    bb_im = q_re[..., None] * bi + q_im[..., None] * br
    h_re = 0.0
    h_im = 0.0
    for t, rev in ((0, False), (1, True)):
        xr = jnp.einsum('bsgc,gpc->bsgp', u, bb_re[t])
        xi = jnp.einsum('bsgc,gpc->bsgp', u, bb_im[t])
        hr, hi = _complex_diag_scan(ab_re[t], ab_im[t], xr, xi, rev)
        h_re = h_re + hr
        h_im = h_im + hi
    y = (jnp.einsum('bsgp,gcp->bsgc', h_re, c_re.astype(f32))
         - jnp.einsum('bsgp,gcp->bsgc', h_im, c_im.astype(f32))
         + d_skip.astype(f32).reshape(C_GROUPS, C_GROUP) * u)
    y = jax.nn.gelu(y.reshape(nb, s, C_WIDTH)).astype(z.dtype)
    return y * jax.nn.sigmoid(y @ w_glu + b_glu)


def trunk(x, p):
    for l in range(DEPTH):
        h = rms_norm(x, p['norm1_g'][l])
        zin = h @ p['w_in'][l]
        za = zin[..., :2 * A_WIDTH]
        zb = zin[..., 2 * A_WIDTH:2 * A_WIDTH + B_WIDTH]
        zc = zin[..., 2 * A_WIDTH + B_WIDTH:]
        ya = gmlp_mixer(za, p['a_v_g'][l], p['a_ws'][l], p['a_bs'][l])
        yb = fourier_mixer(zb)
        yc = s5_mixer(zc, p['c_lam_re'][l], p['c_lam_im'][l], p['c_log_step'][l],
                      p['c_b_re'][l], p['c_b_im'][l], p['c_c_re'][l], p['c_c_im'][l],
                      p['c_d'][l], p['c_w_glu'][l], p['c_b_glu'][l])
        og = p['out_norm_g'][l]
        y = jnp.concatenate([
            rms_norm(ya, og[:A_WIDTH]),
            rms_norm(yb, og[A_WIDTH:A_WIDTH + B_WIDTH]),
            rms_norm(yc, og[A_WIDTH + B_WIDTH:])], axis=-1)
        x = x + y @ p['w_out'][l]
        h = rms_norm(x, p['norm2_g'][l])
        x = x + (jax.nn.silu(h @ p['w_gate'][l]) * (h @ p['w_up'][l])) @ p['w_down'][l]
    return rms_norm(x, p['final_g'])


def setup_inputs(seed: int = 0) -> dict:
    key = jax.random.key(seed)
    k = jax.random.split(key, 24)
    f32 = jnp.float32
    nrm = lambda kk, shape, scale: jax.random.normal(kk, shape, f32) * scale
    n_idx = jnp.arange(C_STATE, dtype=f32)
    return {
        'x_prompt': nrm(k[0], (BATCH, SEQ, D_MODEL), 1.0),
        'x_sample': nrm(k[1], (DEC_BATCH, DEC_SEQ, D_MODEL), 1.0),
        'norm1_g': 1.0 + nrm(k[2], (DEPTH, D_MODEL), 0.02),
        'w_in': nrm(k[3], (DEPTH, D_MODEL, IN_WIDTH), D_MODEL ** -0.5),
        'a_v_g': 1.0 + nrm(k[4], (DEPTH, A_WIDTH), 0.02),
        'a_ws': nrm(k[5], (DEPTH, A_HEADS, CHUNK, CHUNK), 0.5 * CHUNK ** -0.5),
        'a_bs': 1.0 + nrm(k[6], (DEPTH, A_HEADS, CHUNK), 0.02),
        'c_lam_re': -0.5 + nrm(k[7], (DEPTH, 2, C_GROUPS, C_STATE), 0.01),
        'c_lam_im': math.pi * n_idx + nrm(k[8], (DEPTH, 2, C_GROUPS, C_STATE), 0.01),
        'c_log_step': jax.random.uniform(k[9], (DEPTH, 2, C_GROUPS), f32, math.log(1e-3), math.log(1e-1)),
        'c_b_re': nrm(k[10], (DEPTH, C_GROUPS, C_STATE, C_GROUP), (2 * C_GROUP) ** -0.5),
        'c_b_im': nrm(k[11], (DEPTH, C_GROUPS, C_STATE, C_GROUP), (2 * C_GROUP) ** -0.5),
        'c_c_re': nrm(k[12], (DEPTH, C_GROUPS, C_GROUP, C_STATE), C_STATE ** -0.5),
        'c_c_im': nrm(k[13], (DEPTH, C_GROUPS, C_GROUP, C_STATE), C_STATE ** -0.5),
        'c_d': nrm(k[14], (DEPTH, C_WIDTH), 1.0),
        'c_w_glu': nrm(k[15], (DEPTH, C_WIDTH, C_WIDTH), C_WIDTH ** -0.5),
        'c_b_glu': nrm(k[16], (DEPTH, C_WIDTH), 0.02),
        'out_norm_g': 1.0 + nrm(k[17], (DEPTH, MIX_WIDTH), 0.02),
        'w_out': nrm(k[18], (DEPTH, MIX_WIDTH, D_MODEL), MIX_WIDTH ** -0.5),
        'norm2_g': 1.0 + nrm(k[19], (DEPTH, D_MODEL), 0.02),
        'w_gate': nrm(k[20], (DEPTH, D_MODEL, D_FF), D_MODEL ** -0.5),
        'w_up': nrm(k[21], (DEPTH, D_MODEL, D_FF), D_MODEL ** -0.5),
        'w_down': nrm(k[22], (DEPTH, D_FF, D_MODEL), D_FF ** -0.5),
        'final_g': 1.0 + nrm(k[23], (D_MODEL,), 0.02),
    }


def reference(x_prompt, x_sample, norm1_g, w_in, a_v_g, a_ws, a_bs, c_lam_re, c_lam_im,
              c_log_step, c_b_re, c_b_im, c_c_re, c_c_im, c_d, c_w_glu, c_b_glu,
              out_norm_g, w_out, norm2_g, w_gate, w_up, w_down, final_g):
    p = dict(norm1_g=norm1_g, w_in=w_in, a_v_g=a_v_g, a_ws=a_ws, a_bs=a_bs,
             c_lam_re=c_lam_re, c_lam_im=c_lam_im, c_log_step=c_log_step,
             c_b_re=c_b_re, c_b_im=c_b_im, c_c_re=c_c_re, c_c_im=c_c_im, c_d=c_d,
             c_w_glu=c_w_glu, c_b_glu=c_b_glu, out_norm_g=out_norm_g, w_out=w_out,
             norm2_g=norm2_g, w_gate=w_gate, w_up=w_up, w_down=w_down, final_g=final_g)
    y_prompt = trunk(x_prompt, p)
    y_sample = trunk(x_sample, p)
    return (y_prompt, y_sample)
```

```python
import math
from contextlib import ExitStack

import numpy as np
import ml_dtypes
import concourse.bass as bass
import concourse.mybir as mybir
from concourse.bass_utils import run_bass_kernel_spmd

F32 = mybir.dt.float32
BF16 = mybir.dt.bfloat16
I32 = mybir.dt.int32
AF = mybir.ActivationFunctionType
ALU = mybir.AluOpType
AX = mybir.AxisListType

D = 2048
KC = 16
NL = 2
AW = 768
BW = 512
CW = 768
INW = 2816
DFF = 5632
FCH = 44
NG = 48
EPS = 1e-6
TT = 512
TWO_PI = 2.0 * math.pi


class Buf:
    __slots__ = ("name", "w", "r")

    def __init__(self, name=""):
        self.name = name
        self.w = None
        self.r = {}


class Eng:
    def __init__(self, name, h, sem):
        self.name = name
        self.h = h
        self.sem = sem
        self.cnt = 0
        self.seen = {}


class DmaQ:
    def __init__(self, name, eng, sems):
        self.name = name
        self.eng = eng
        self.sems = sems
        self.cnts = [0] * len(sems)
        self.i = 0


class Tracker:
    def __init__(self, nc, stack):
        self.nc = nc
        mk = lambda n: stack.enter_context(nc.semaphore(n))
        self.E = {
            "pe": Eng("pe", nc.tensor, mk("s_pe")),
            "act": Eng("act", nc.scalar, mk("s_act")),
            "dve": Eng("dve", nc.vector, mk("s_dve")),
            "pool": Eng("pool", nc.gpsimd, mk("s_pool")),
            "sp": Eng("sp", nc.sync, mk("s_sp")),
        }
        self.Q = {
            "sp": DmaQ("sp", self.E["sp"], [mk(f"q_sp{i}") for i in range(16)]),
            "pool": DmaQ("pool", self.E["pool"], [mk(f"q_pool{i}") for i in range(12)]),
            "act": DmaQ("act", self.E["act"], [mk(f"q_act{i}") for i in range(8)]),
        }

    @staticmethod
    def _key(sem):
        return sem.num

    def _collect(self, eng_name, reads, writes):
        deps = []
        for b in reads:
            if b.w is not None:
                deps.append(b.w)
        skip = eng_name if eng_name == "pe" else None
        for b in writes:
            if b.w is not None and b.w[2] != skip:
                deps.append(b.w)
            for t in b.r.values():
                if t[2] != skip:
                    deps.append(t)
        return deps

    def _wait(self, E, deps):
        need = {}
        for sem, val, _src in deps:
            k = self._key(sem)
            if k not in need or need[k][1] < val:
                need[k] = (sem, val)
        for k, (sem, val) in need.items():
            if E.seen.get(k, 0) >= val:
                continue
            E.h.wait_ge(sem, val)
            E.seen[k] = val

    def _record(self, tok, reads, writes):
        k = self._key(tok[0])
        for b in writes:
            b.w = tok
            b.r = {}
        for b in reads:
            b.r[k] = tok

    def op(self, eng, fn, reads=(), writes=(), inc=True):
        E = self.E[eng]
        self._wait(E, self._collect(eng, reads, writes))
        ins = fn(E.h)
        if inc:
            ins.then_inc(E.sem, 1)
            E.cnt += 1
            tok = (E.sem, E.cnt, eng)
        else:
            tok = (E.sem, E.cnt + 1, eng)
        self._record(tok, reads, writes)
        return ins

    def dma(self, q, out, in_, reads=(), writes=(), wait_bufs=()):
        Q = self.Q[q]
        E = Q.eng
        i = Q.i
        Q.i = (Q.i + 1) % len(Q.sems)
        deps = self._collect(None, reads, writes)
        for b in wait_bufs:
            if b.w is not None:
                deps.append(b.w)
        if Q.cnts[i] > 0:
            deps.append((Q.sems[i], Q.cnts[i], "dma"))
        self._wait(E, deps)
        E.h.dma_start(out=out, in_=in_).then_inc(Q.sems[i], 16)
        Q.cnts[i] += 16
        self._record((Q.sems[i], Q.cnts[i], "dma"), reads, writes)

    def barrier(self, final=False):
        toks = []
        for E in self.E.values():
            if E.cnt > 0:
                toks.append((E.sem, E.cnt, E.name))
        for Q in self.Q.values():
            if Q.name == "pool" and not final:
                continue
            for s, c in zip(Q.sems, Q.cnts):
                if c > 0:
                    toks.append((s, c, "dma"))
        for E in self.E.values():
            self._wait(E, [t for t in toks if t[2] != E.name or False])


class Builder:
    def __init__(self, S):
        self.S = S
        self.NT = S // TT
        self.NCH = S // 8
        self.NC = S // 128
        self.nc = bass.Bass("TRN2", target_bir_lowering=False)
        self.root = ExitStack()
        self.T = Tracker(self.nc, self.root)
        self.bufs = {}
        self.ring_i = 0
        self.pending_prep = []

    def B(self, name):
        b = self.bufs.get(name)
        if b is None:
            b = self.bufs[name] = Buf(name)
        return b

    def BL(self, name, n):
        return [self.B(f"{name}_{i}") for i in range(n)]

    def din(self, name, shape, dt=F32):
        return self.nc.dram_tensor(name, list(shape), dt, kind="ExternalInput").ap()

    def dscr(self, name, shape, dt):
        return self.nc.dram_tensor(name, list(shape), dt, kind="Internal").ap()

    def sb(self, stack, name, shape, dt):
        return stack.enter_context(self.nc.sbuf_tensor("sb_" + name, list(shape), dt))

    def mm(self, out, lhsT, rhs, start, stop, reads, writes, inc=None):
        if inc is None:
            inc = stop
        return self.T.op("pe", lambda h: h.matmul(out, lhsT, rhs, start=start, stop=stop),
                         reads, writes, inc=inc)

    def act(self, out, in_, func, reads, writes, bias=None, scale=1.0, eng="act"):
        kw = {}
        if bias is not None:
            kw["bias"] = bias
        return self.T.op("act", lambda h: h.activation(out=out, in_=in_, func=func, scale=scale, **kw),
                         reads, writes)

    def tt(self, eng, out, in0, in1, op, reads, writes):
        return self.T.op(eng, lambda h: h.tensor_tensor(out=out, in0=in0, in1=in1, op=op), reads, writes)

    def ts(self, eng, out, in0, s1, s2, op0, op1, reads, writes):
        if op1 is None:
            return self.T.op(eng, lambda h: h.tensor_scalar(out=out, in0=in0, scalar1=s1, scalar2=None, op0=op0),
                             reads, writes)
        return self.T.op(eng, lambda h: h.tensor_scalar(out=out, in0=in0, scalar1=s1, scalar2=s2, op0=op0, op1=op1),
                         reads, writes)

    def stt(self, eng, out, in0, scalar, in1, op0, op1, reads, writes):
        return self.T.op(eng, lambda h: h.scalar_tensor_tensor(out=out, in0=in0, scalar=scalar, in1=in1,
                                                               op0=op0, op1=op1), reads, writes)

    def copy(self, eng, out, in_, reads, writes):
        if eng == "act":
            return self.T.op("act", lambda h: h.copy(out=out, in_=in_), reads, writes)
        return self.T.op(eng, lambda h: h.tensor_copy(out=out, in_=in_), reads, writes)

    def memset(self, eng, ap, val, writes):
        return self.T.op(eng, lambda h: h.memset(ap, val), (), writes)

    def prep_weight(self, name, src, kcn, slabs, defer=False):
        out = []
        srcv = src.rearrange("(kc p) n -> p kc n", p=128)
        for si, (c0, w) in enumerate(slabs):
            scr = self.dscr(f"{name}_s{si}", [128, kcn, w], BF16)
            b = self.B(f"{name}_s{si}")
            thunk = (lambda after=(), scr=scr, c0=c0, w=w, b=b: self.T.dma("pool", scr, srcv[:, :, c0:c0 + w], reads=(), writes=(b,), wait_bufs=tuple(after)))
            if defer:
                b.w = ("pending",)
                self.pending_prep.append((b, thunk))
            else:
                thunk()
            out.append((scr, b, w, kcn))
        return out

    def issue_prep(self, n=None, after=()):
        k = 0
        while self.pending_prep and (n is None or k < n):
            b, thunk = self.pending_prep.pop(0)
            b.w = None
            thunk(after)
            k += 1

    def load_slab(self, slab):
        scr, b, w, kcn = slab
        while b.w == ("pending",):
            self.issue_prep(1)
        i = self.ring_i
        self.ring_i = (self.ring_i + 1) % len(self.ring)
        t = self.ring[i]
        rb = self.B(f"ring{i}")
        view = t[:, 0:kcn * w].rearrange("p (k c) -> p k c", c=w)
        self.T.dma("sp", view, scr, reads=(b,), writes=(rb,))
        return view, rb

    def rstd_from_sq(self, sq_list, nfeat, bank, rstd_ap, rstd_buf, sq_bufs):
        n = len(sq_list)
        pb = self.pb[bank]
        N = sq_list[0].shape[-1]
        for i, s in enumerate(sq_list):
            self.mm(self.ps[:, bank, 0:N], self.ones_bf[:], s, i == 0, i == n - 1,
                    reads=(self.B("consts"),) + tuple(sq_bufs), writes=(pb,))
        self.act(rstd_ap, self.ps[:, bank, 0:N], AF.Sqrt, reads=(pb, self.B("consts")), writes=(rstd_buf,),
                 bias=self.epscol[:, 0:1], scale=1.0 / nfeat)
        self.T.op("dve", lambda h: h.reciprocal(out=rstd_ap, in_=rstd_ap), (rstd_buf,), (rstd_buf,))

    def build(self):
        nc = self.nc
        S, NT, NCH, NC = self.S, self.NT, self.NCH, self.NC
        T = self.T
        root = self.root
        x_in = self.din("x", [S, D])
        w_in = self.din("w_in", [NL, D, INW])
        w_out = self.din("w_out", [NL, D, D])
        w_gate = self.din("w_gate", [NL, D, DFF])
        w_up = self.din("w_up", [NL, D, DFF])
        w_down = self.din("w_down", [NL, DFF, D])
        w_glu = self.din("w_glu", [NL, CW, CW])
        ws_t = self.din("ws_t", [NL, 128, 6, 128])
        vecs = self.din("vecs", [128, 124])
        vg_rep = self.din("vg_rep", [NL, 128, AW])
        bs_rep = self.din("bs_rep", [NL, 128, AW])
        ogb_rep = self.din("ogb_rep", [NL, 128, BW])
        lam_d = self.din("lam", [NL, 128, 2, 2, NG])
        lstep_d = self.din("lstep", [NL, 128, 2, NG])
        bc_d = self.din("bc", [NL, 128, 4, NG, 16])
        cd_d = self.din("cd_rep", [NL, 128, NG])
        ident_d = self.din("ident", [128, 128])
        cmat_d = self.din("cmat", [128, 2, 128], BF16)
        masks_d = self.din("masks", [128, 3, 128])
        dft128_d = self.din("dft128", [128, 2, 128], BF16)
        dftS_d = self.din("dftS", [2, NC, 128, NC, 128], BF16)
        kvec_d = self.din("kvecs", [128, 3, 9])
        iota_d = self.din("iota", [128, NCH])
        y_out = nc.dram_tensor("y", [S, D], F32, kind="ExternalOutput").ap()

        XT = [self.dscr(f"XT{i}", [KC, 128, S], F32) for i in range(2)]
        YT = self.dscr("YT", [KC, 128, S], BF16)
        YB = self.dscr("YB", [S, BW], BF16)
        PQ = self.dscr("PQ", [S, 1024], BF16)
        ZC2 = self.dscr("ZC2", [CW, 8, NCH], BF16)
        YC2 = self.dscr("YC2", [CW, 8, NCH], BF16)

        self.ps = nc.alloc_psum_tensor("ps", [128, 8, 512], F32)
        self.pb = [self.B(f"psum{i}") for i in range(8)]
        P = root
        ident = self.sb(P, "ident", [128, 128], F32)
        dft128 = self.sb(P, "dft128", [128, 2, 128], BF16)
        vec = self.sb(P, "vec", [128, 124], F32)
        self.ones_bf = self.sb(P, "ones_bf", [128, 128], BF16)
        self.epscol = self.sb(P, "epscol", [128, 2], F32)
        ident_bf = self.sb(P, "ident_bf", [128, 128], BF16)
        cb = self.B("consts")
        for dst, src in ((ident, ident_d), (dft128, dft128_d), (vec, vecs)):
            T.dma("sp", dst[:], src, reads=(), writes=(cb,))
        self.memset("dve", self.ones_bf[:], 1.0, (cb,))
        self.memset("dve", self.epscol[:, 0:1], EPS, (cb,))
        self.memset("dve", self.epscol[:, 1:2], math.pi / 2.0, (cb,))
        self.copy("dve", ident_bf[:], ident[:], (cb,), (cb,))

        def vcol(kind, l, k):
            base = {"n1g": 0, "n2g": 32, "og": 64, "fg": 96, "bglu": 112}[kind]
            per = {"n1g": 16, "n2g": 16, "og": 16, "fg": 0, "bglu": 6}[kind]
            c = base + l * per + k
            return vec[:, c:c + 1]

        self.wsT_all = [self.sb(P, f"wsT{l}", [128, 6, 128], BF16) for l in range(NL)]
        for l in range(NL):
            T.dma("pool", self.wsT_all[l][:], ws_t[l], reads=(), writes=(self.B("wsT"),))
        fm_slabs = [(0, 512), (512, 256), (1536, 512), (2048, 512), (2560, 256)]
        v_slabs = [(768, 384), (1152, 384)]

        def prep_layer(l, defer_rest):
            w = {}
            w["in_fm"] = self.prep_weight(f"win{l}", w_in[l], KC, fm_slabs, defer=(l > 0))
            w["in_v"] = self.prep_weight(f"winv{l}", w_in[l], KC, v_slabs, defer=(l > 0))
            w["glu"] = self.prep_weight(f"wglu{l}", w_glu[l], 6, [(0, CW)], defer=(l > 0))
            w["out"] = self.prep_weight(f"wout{l}", w_out[l], KC, [(i * 512, 512) for i in range(4)], defer=defer_rest)
            g = self.prep_weight(f"wg{l}", w_gate[l], KC, [(i * 512, 512) for i in range(11)], defer=defer_rest)
            u_ = self.prep_weight(f"wu{l}", w_up[l], KC, [(i * 512, 512) for i in range(11)], defer=defer_rest)
            w["gate"], w["up"] = g, u_
            w["down"] = self.prep_weight(f"wd{l}", w_down[l], FCH, [(i * 128, 128) for i in range(16)], defer=defer_rest)
            return w
        W = [prep_layer(0, True)]

        for l in range(NL):
            last = (l == NL - 1)
            self.phase1(l, x_in if l == 0 else None, XT, YT, PQ, ZC2, W[l], ws_t, vg_rep, bs_rep,
                        ident, dft128, vcol)
            T.barrier()
            self.phase2_s5(l, ZC2, YC2, lam_d, lstep_d, bc_d, cd_d, cmat_d, masks_d, kvec_d, iota_d)
            T.barrier()
            self.phase2_glu(l, YC2, YT, W[l], vcol)
            T.barrier()
            self.phase2_fnet(l, PQ, YB, dftS_d, ogb_rep)
            T.barrier()
            self.issue_prep()
            if l + 1 < NL:
                W.append(prep_layer(l + 1, True))
            self.phase3(l, XT[l % 2], XT[(l + 1) % 2], YT, YB, W[l], ident, ident_bf, vcol, last, y_out)
            T.barrier()
        T.barrier(final=True)
        root.close()
        return nc

    def phase1(self, l, x_in, XT, YT, PQ, ZC2, W, ws_t, vg_rep, bs_rep, ident, dft128, vcol):
        nc, T, S, NT, NCH = self.nc, self.T, self.S, self.NT, self.NCH
        cb = self.B("consts")
        with ExitStack() as st:
            xin = self.sb(st, f"p1xin{l}", [128, 2, D], F32) if x_in is not None else None
            xT = self.sb(st, f"p1xT{l}", [128, KC, TT], F32)
            scA = self.sb(st, f"p1scA{l}", [128, KC * TT], BF16)
            hT = self.sb(st, f"p1hT{l}", [128, KC, TT], BF16)
            rstd = self.sb(st, f"p1rstd{l}", [128, TT], F32)
            rstd2 = self.sb(st, f"p1rstd2{l}", [128, TT], F32)
            hT2 = self.sb(st, f"p1hT2{l}", [128, KC, TT], BF16)
            self.ring = [self.sb(st, f"p1ring{l}_{i}", [128, KC * 512], BF16) for i in range(3)]
            u = self.sb(st, f"p1u{l}", [128, 6, TT], F32)
            v2 = self.sb(st, f"p1v2{l}", [128, 2, AW], F32)
            vsq = self.sb(st, f"p1vsq{l}", [128, AW], F32)
            vss = self.sb(st, f"p1vss{l}", [128, 2, 8], F32)
            vn = self.sb(st, f"p1vn{l}", [128, 2, AW], BF16)
            ya = self.sb(st, f"p1ya{l}", [128, 6, TT], F32)
            pq = self.sb(st, f"p1pq{l}", [128, 2, 1024], BF16)
            zcd = self.sb(st, f"p1zcd{l}", [128, 6, 8, 64], BF16)
            wsT = self.wsT_all[l]
            vg = self.sb(st, f"p1vg{l}", [128, AW], F32)
            bsr = self.sb(st, f"p1bs{l}", [128, AW], F32)
            lc = self.B(f"p1consts{l}")
            T.dma("sp", vg[:], vg_rep[l], reads=(), writes=(lc,))
            T.dma("sp", bsr[:], bs_rep[l], reads=(), writes=(lc,))
            sq = scA[:].rearrange("p (k t) -> p k t", t=TT)
            yasq = scA[:, 0:3072].rearrange("p (k t) -> p k t", t=TT)
            yan = scA[:, 3072:6144].rearrange("p (k t) -> p k t", t=TT)
            zbT = scA[:, 6144:8192].rearrange("p (k t) -> p k t", t=TT)
            bxT, bscA, bhT, brstd = self.B("p1xT"), self.B("p1scA"), self.B("p1hT"), self.B("p1rstd")
            bu, bya, bpq, bzcd = self.B("p1u"), self.B("p1ya"), self.B("p1pq"), self.B("p1zcd")
            bv2 = [self.B("p1v2_0"), self.B("p1v2_1")]
            bvn = [self.B("p1vn_0"), self.B("p1vn_1")]
            bvsq, bvss = self.B("p1vsq"), self.B("p1vss")
            bxin = [self.B("p1xin0"), self.B("p1xin1")]
            bXT_prev = self.B("XTa") if l % 2 == 0 else self.B("XTb")
            XTl = XT[l % 2]
            ps, pb = self.ps, self.pb
            hTs = [hT, hT2]
            bxTs = self.BL("p1xT", KC)
            bscs = self.BL("p1sc", KC)
            bhTk = [self.BL("p1hTa", KC), self.BL("p1hTb", KC)]
            bus = self.BL("p1u", 6)
            byas = self.BL("p1ya", 6)
            bzcds = self.BL("p1zcd", 6)
            bpqs = [self.B("p1pq0"), self.B("p1pq1")]
            bhTs = [bhT, self.B("p1hT2")]
            brstd2 = self.B("p1rstd2")

            def stage_A(t):
                t0 = t * TT
                if x_in is not None:
                    for n in range(4):
                        xb = n % 2
                        T.dma("sp", xin[:, xb, :], x_in[t0 + n * 128:t0 + (n + 1) * 128, :],
                              reads=(), writes=(bxin[xb],))
                        for kg in range(4):
                            bank = 4 + (n * 4 + kg) % 2
                            for kk in range(4):
                                k = kg * 4 + kk
                                T.op("pe", lambda h, k=k, kk=kk, bank=bank, xb=xb: h.transpose(
                                    ps[:, bank, kk * 128:(kk + 1) * 128], xin[:, xb, k * 128:(k + 1) * 128], ident[:]),
                                    reads=(bxin[xb], cb), writes=(pb[bank],), inc=(kk == 3))
                            self.copy("act" if kg % 2 == 0 else "dve",
                                      xT[:, kg * 4:(kg + 1) * 4, n * 128:(n + 1) * 128],
                                      ps[:, bank, :].rearrange("p (k t) -> p k t", t=128),
                                      (pb[bank],), tuple(bxTs[kg * 4:(kg + 1) * 4]))
                    T.dma("act", XTl[:, :, t0:t0 + TT].rearrange("k p t -> p k t"), xT[:],
                          reads=tuple(bxTs), writes=())
                else:
                    T.dma("sp", xT[:], XTl[:, :, t0:t0 + TT].rearrange("k p t -> p k t"),
                          reads=(), writes=tuple(bxTs))

            def stage_Bsq(t):
                for k in range(KC):
                    self.act(sq[:, k, :], xT[:, k, :], AF.Square, (bxTs[k],), (bscs[k],))

            def stage_Brs(t):
                self.rstd_from_sq([sq[:, k, :] for k in range(KC)], D, 6, rstd[:], brstd, tuple(bscs))

            def stage_Bap(t, k0, k1):
                hb = t % 2
                for k in range(k0, k1):
                    self.stt("dve", hTs[hb][:, k, :], xT[:, k, :], vcol("n1g", l, k), rstd[:], ALU.mult, ALU.mult,
                             (bxTs[k], brstd, cb), (bhTk[hb][k],))

            def stage_C(t):
                t0 = t * TT
                hb = t % 2
                mchunks = []
                for si, slab in enumerate(W["in_fm"]):
                    for j in range(slab[2] // 128):
                        mchunks.append((si, j))
                views = {}
                for mi, (si, j) in enumerate(mchunks):
                    if si not in views:
                        views = {si: self.load_slab(W["in_fm"][si])}
                    wv, rb = views[si]
                    bank = mi % 4
                    for k in range(KC):
                        self.mm(ps[:, bank, :], wv[:, k, j * 128:(j + 1) * 128], hTs[hb][:, k, :], k == 0, k == KC - 1,
                                reads=(rb, bhTk[hb][k]), writes=(pb[bank],))
                    if mi < 6:
                        self.act(u[:, mi, :], ps[:, bank, :], AF.Gelu_apprx_tanh, (pb[bank],), (bus[mi],))
                    elif mi < 10:
                        self.copy("dve", zbT[:, mi - 6, :], ps[:, bank, :], (pb[bank],), (bscs[12 + mi - 6],))
                    else:
                        m = mi - 10
                        self.copy("act" if m % 2 == 0 else "dve", zcd[:, m, :, :],
                                  ps[:, bank, :].rearrange("p (c j) -> p j c", j=8), (pb[bank],), (bzcds[m],))
                c0 = t0 // 8
                for fc in range(6):
                    T.dma("act", ZC2[fc * 128:(fc + 1) * 128, :, c0:c0 + 64], zcd[:, fc, :, :], reads=(bzcds[fc],), writes=())

            def stage_D(t):
                t0 = t * TT
                for n in range(4):
                    bP, bQ = (4, 5) if n % 2 == 0 else (6, 7)
                    for h4 in range(4):
                        self.mm(ps[:, bP, h4 * 128:(h4 + 1) * 128], zbT[:, h4, n * 128:(n + 1) * 128], dft128[:, 0, :],
                                True, True, reads=(bscs[12 + h4], cb), writes=(pb[bP],), inc=(h4 == 3))
                    for h4 in range(4):
                        self.mm(ps[:, bQ, h4 * 128:(h4 + 1) * 128], zbT[:, h4, n * 128:(n + 1) * 128], dft128[:, 1, :],
                                True, True, reads=(bscs[12 + h4], cb), writes=(pb[bQ],), inc=(h4 == 3))
                    self.copy("act", pq[:, n % 2, 0:512], ps[:, bP, :], (pb[bP],), (bpqs[n % 2],))
                    self.copy("dve", pq[:, n % 2, 512:1024], ps[:, bQ, :], (pb[bQ],), (bpqs[n % 2],))
                    T.dma("act", PQ[t0 + n * 128:t0 + (n + 1) * 128, :], pq[:, n % 2, :], reads=(bpqs[n % 2],), writes=())

            def stage_E1(t, n, vviews):
                hb = t % 2
                vb = n % 2
                for half in range(2):
                    wv, rb = vviews[half]
                    bank = 2 * half + (0 if n % 2 == 0 else 1)
                    for k in range(KC):
                        self.mm(ps[:, bank, 0:384], hTs[hb][:, k, n * 128:(n + 1) * 128], wv[:, k, :], k == 0, k == KC - 1,
                                reads=(rb, bhTk[hb][k]), writes=(pb[bank],))
                    self.act(v2[:, vb, half * 384:(half + 1) * 384], ps[:, bank, 0:384], AF.Gelu_apprx_tanh,
                             (pb[bank],), (bv2[vb],))

            def stage_E2(t, n):
                vb = n % 2
                self.tt("dve", vsq[:], v2[:, vb, :], v2[:, vb, :], ALU.mult, (bv2[vb],), (bvsq,))
                T.op("dve", lambda h: h.tensor_reduce(out=vss[:, vb, 0:6], in_=vsq[:].rearrange("p (h d) -> p h d", d=128),
                                                      axis=AX.X, op=ALU.add), (bvsq,), (bvss,))
                self.act(vss[:, vb, 0:6], vss[:, vb, 0:6], AF.Sqrt, (bvss, cb), (bvss,), bias=self.epscol[:, 0:1],
                         scale=1.0 / 128.0)
                T.op("dve", lambda h: h.reciprocal(out=vss[:, vb, 0:6], in_=vss[:, vb, 0:6]), (bvss,), (bvss,))
                rs_b = bass.AP(vss, vb * 8, [[16, 128], [1, 6], [0, 128]])
                self.tt("dve", vsq[:].rearrange("p (h d) -> p h d", d=128), v2[:, vb, :].rearrange("p (h d) -> p h d", d=128),
                        rs_b, ALU.mult, (bv2[vb], bvss), (bvsq,))
                self.tt("dve", vn[:, vb, :], vsq[:], vg[:], ALU.mult, (bvsq, lc), (bvn[vb],))
                for hh in range(6):
                    bk = (6 if hh < 4 else 7) if n % 2 == 0 else (4 if hh < 4 else 5)
                    col = (hh % 4) * 128
                    self.mm(ps[:, bk, col:col + 128], vn[:, vb, hh * 128:(hh + 1) * 128], wsT[:, hh, :], True, True,
                            reads=(bvn[vb], self.B("wsT")), writes=(pb[bk],), inc=(hh == 3 or hh == 5))
                bkA, bkB = (6, 7) if n % 2 == 0 else (4, 5)
                for (bk, h0, nh) in ((bkA, 0, 4), (bkB, 4, 2)):
                    o = ya[:, h0:h0 + nh, n * 128:(n + 1) * 128]
                    self.tt("dve", o, ps[:, bk, 0:nh * 128].rearrange("p (h q) -> p h q", q=128),
                            bsr[:, h0 * 128:(h0 + nh) * 128].rearrange("p (h q) -> p h q", q=128), ALU.add,
                            (pb[bk], lc), tuple(byas[h0:h0 + nh]))
                    self.tt("dve", o, o, u[:, h0:h0 + nh, n * 128:(n + 1) * 128], ALU.mult,
                            tuple(byas[h0:h0 + nh]) + tuple(bus[h0:h0 + nh]), tuple(byas[h0:h0 + nh]))

            def stage_F(t):
                t0 = t * TT
                for m in range(6):
                    self.act(yasq[:, m, :], ya[:, m, :], AF.Square, (byas[m],), (bscs[m],))
                self.rstd_from_sq([yasq[:, m, :] for m in range(6)], AW, 3, rstd2[:], brstd2, tuple(bscs[0:6]))
                for m in range(6):
                    self.stt("dve", yan[:, m, :], ya[:, m, :], vcol("og", l, m), rstd2[:], ALU.mult, ALU.mult,
                             (byas[m], brstd2, cb), (bscs[6 + m],))
                T.dma("act", YT[0:6, :, t0:t0 + TT].rearrange("k p t -> p k t"), yan, reads=tuple(bscs[6:12]), writes=())

            stage_A(0)
            stage_Bsq(0)
            stage_Brs(0)
            stage_Bap(0, 0, KC)
            for t in range(NT):
                nxt = t + 1 < NT
                stage_C(t)
                if nxt:
                    stage_A(t + 1)
                stage_D(t)
                if nxt:
                    stage_Bsq(t + 1)
                vviews = [self.load_slab(s_) for s_ in W["in_v"]]
                stage_E1(t, 0, vviews)
                stage_E1(t, 1, vviews)
                if nxt:
                    stage_Brs(t + 1)
                stage_E2(t, 0)
                if nxt:
                    stage_Bap(t + 1, 0, 4)
                stage_E1(t, 2, vviews)
                stage_E2(t, 1)
                if nxt:
                    stage_Bap(t + 1, 4, 8)
                stage_E1(t, 3, vviews)
                stage_E2(t, 2)
                if nxt:
                    stage_Bap(t + 1, 8, 12)
                stage_E2(t, 3)
                if nxt:
                    stage_Bap(t + 1, 12, 16)
                stage_F(t)

    def phase2_s5(self, l, ZC2, YC2, lam_d, lstep_d, bc_d, cd_d, cmat_d, masks_d, kvec_d, iota_d):
        nc, T, S, NCH = self.nc, self.T, self.S, self.NCH
        cb = self.B("consts")
        ps, pb = self.ps, self.pb
        with ExitStack() as st:
            lam = self.sb(st, f"s5lam{l}", [128, 2, 2, NG], F32)
            lstep = self.sb(st, f"s5lstep{l}", [128, 2, NG], F32)
            bc = self.sb(st, f"s5bc{l}", [128, 4, NG, 16], F32)
            cdr = self.sb(st, f"s5cd{l}", [128, NG], F32)
            sc = self.sb(st, f"s5sc{l}", [128, 16, 2, NG], F32)
            bbar = self.sb(st, f"s5bbar{l}", [128, 2, 2, NG, 16], F32)
            tmpb = self.sb(st, f"s5tmpb{l}", [128, 2, NG * 16], F32)
            cyk = self.sb(st, f"s5cyk{l}", [128, NG, 9], F32)
            cyo = self.sb(st, f"s5cyo{l}", [128, NG, 9], F32)
            cyi = self.sb(st, f"s5cyi{l}", [128, NG, 9], I32)
            cyk2 = self.sb(st, f"s5cyk2{l}", [128, NG, 9], F32)
            magk = self.sb(st, f"s5magk{l}", [128, NG, 9], F32)
            pv = self.sb(st, f"s5pv{l}", [128, 12, NG, 9], F32)
            r8 = self.sb(st, f"s5r8{l}", [128, 2, NG], F32)
            cy8 = self.sb(st, f"s5cy8{l}", [128, 2, NG], F32)
            cy8i = self.sb(st, f"s5cy8i{l}", [128, 2, NG], I32)
            tab = self.sb(st, f"s5tab{l}", [128, 2, 8, 8, 144], BF16)
            tmp1 = self.sb(st, f"s5tmp1{l}", [128, 8, 144], F32)
            tmp2 = self.sb(st, f"s5tmp2{l}", [128, 8, 144], F32)
            tmp3 = self.sb(st, f"s5tmp3{l}", [128, 8, 144], F32)
            tmp4 = self.sb(st, f"s5tmp4{l}", [128, 8, 144], F32)
            t0sb = self.sb(st, f"s5t0sb{l}", [128, 256], F32)
            t0s2 = self.sb(st, f"s5t0s2{l}", [128, 128], F32)
            t0s3 = self.sb(st, f"s5t0s3{l}", [128, 128], F32)
            bt0b, bt0c = self.B("s5t0b"), self.B("s5t0c")
            T0 = self.sb(st, f"s5T0{l}", [128, 2, 128], BF16)
            Wsb = self.sb(st, f"s5W{l}", [128, 2, 512], BF16)
            U = self.sb(st, f"s5U{l}", [128, 2, NCH], BF16)
            yv = self.sb(st, f"s5yv{l}", [128, 2, NCH], F32)
            yi = self.sb(st, f"s5yi{l}", [128, 2, NCH], I32)
            fr = self.sb(st, f"s5fr{l}", [128, 2, NCH], F32)
            nab = self.sb(st, f"s5nab{l}", [128, 2, NCH], F32)
            cosT = self.sb(st, f"s5cos{l}", [128, 2, 2, NCH], F32)
            sinT = self.sb(st, f"s5sin{l}", [128, 2, 2, NCH], F32)
            btmp = self.B("s5tbtmp")
            byv, byi, bfr, bnab = self.B("s5yv"), self.B("s5yi"), self.B("s5fr"), self.B("s5nab")
            bm1, bm2, bgp = self.BL("s5m1", 2), self.BL("s5m2", 2), self.BL("s5gp", 2)
            m1 = self.sb(st, f"s5m1{l}", [128, 2, NCH], F32)
            m2 = self.sb(st, f"s5m2{l}", [128, 2, NCH], F32)
            gp = self.sb(st, f"s5gp{l}", [128, 2, NCH], F32)
            gs = self.sb(st, f"s5gs{l}", [128, 2, 2, NCH], F32)
            Hc = self.sb(st, f"s5Hc{l}", [128, 2, 2, NCH + 1], BF16)
            Hs = self.sb(st, f"s5Hs{l}", [128, 2, 2, NCH + 1], BF16)
            yg = self.sb(st, f"s5yg{l}", [128, 2, NCH], BF16)
            bset = self.B("s5setup")
            cmat = self.sb(st, f"s5cmat{l}", [128, 2, 128], BF16)
            masks = self.sb(st, f"s5masks{l}", [128, 3, 128], F32)
            kvec = self.sb(st, f"s5kvec{l}", [128, 3, 9], F32)
            iota = self.sb(st, f"s5iota{l}", [128, NCH], F32)
            bcst = self.B("s5consts")
            for dst, src in ((cmat, cmat_d), (masks, masks_d), (kvec, kvec_d), (iota, iota_d)):
                T.dma("sp", dst[:], src, (), (bcst,))
            T.dma("sp", lam[:], lam_d[l], (), (bset,))
            T.dma("sp", lstep[:], lstep_d[l], (), (bset,))
            T.dma("sp", bc[:], bc_d[l], (), (bset,))
            T.dma("sp", cdr[:], cd_d[l], (), (bset,))
            R = (bset, cb, bcst)
            Wr = (bset,)
            lamr, lami = lam[:, 0], lam[:, 1]
            A = lambda i: sc[:, i]
            dl, lrd, lid, mag1, cyc1, t_a, t_b, abr, abi, den, nr, qre, qim = [A(i) for i in range(13)]
            sci = cy8i
            self.act(dl, lstep[:], AF.Exp, R, Wr)
            self.tt("dve", lrd, lamr, dl, ALU.mult, R, Wr)
            self.tt("dve", lid, lami, dl, ALU.mult, R, Wr)
            self.act(mag1, lrd, AF.Exp, R, Wr)
            self.ts("dve", cyc1, lid, 1.0 / TWO_PI, None, ALU.mult, None, R, Wr)

            def sincos(out_s, out_c, y_ap, tmp_i, tmp_f, tmp_n):
                self.copy("dve", tmp_i, y_ap, R, Wr)
                self.copy("dve", tmp_n, tmp_i, R, Wr)
                self.tt("dve", tmp_f, y_ap, tmp_n, ALU.subtract, R, Wr)
                self.act(tmp_n, tmp_f, AF.Abs, R, Wr)
                self.act(out_s, tmp_f, AF.Sin, R, Wr, scale=TWO_PI)
                self.act(out_c, tmp_n, AF.Sin, R, Wr, bias=self.epscol[:, 1:2], scale=-TWO_PI)

            sincos(t_a, t_b, cyc1, sci[:], A(13), A(14))
            self.tt("dve", abr, mag1, t_b, ALU.mult, R, Wr)
            self.tt("dve", abi, mag1, t_a, ALU.mult, R, Wr)
            self.tt("dve", den, lamr, lamr, ALU.mult, R, Wr)
            self.tt("dve", t_a, lami, lami, ALU.mult, R, Wr)
            self.tt("dve", den, den, t_a, ALU.add, R, Wr)
            T.op("dve", lambda h: h.reciprocal(out=den, in_=den), R, Wr)
            self.ts("dve", nr, abr, -1.0, None, ALU.add, None, R, Wr)
            self.tt("dve", t_a, nr, lamr, ALU.mult, R, Wr)
            self.tt("dve", t_b, abi, lami, ALU.mult, R, Wr)
            self.tt("dve", t_a, t_a, t_b, ALU.add, R, Wr)
            self.tt("dve", qre, t_a, den, ALU.mult, R, Wr)
            self.tt("dve", t_a, abi, lamr, ALU.mult, R, Wr)
            self.tt("dve", t_b, nr, lami, ALU.mult, R, Wr)
            self.tt("dve", t_a, t_a, t_b, ALU.subtract, R, Wr)
            self.tt("dve", qim, t_a, den, ALU.mult, R, Wr)
            self.act(r8[:], lrd, AF.Exp, R, Wr, scale=8.0)
            self.ts("dve", cy8[:], cyc1, 8.0, None, ALU.mult, None, R, Wr)
            self.copy("dve", cy8i[:], cy8[:], R, Wr)
            self.copy("dve", A(15), cy8i[:], R, Wr)
            self.tt("dve", cy8[:], cy8[:], A(15), ALU.subtract, R, Wr)
            bre, bim, cre, cim = bc[:, 0], bc[:, 1], bc[:, 2], bc[:, 3]

            def qb(qidx, t):
                return bass.AP(sc, qidx * 2 * NG + t * NG, [[16 * 2 * NG, 128], [1, NG], [0, 16]])
            tb0 = tmpb[:, 0, :].rearrange("p (g c) -> p g c", c=16)
            tb1 = tmpb[:, 1, :].rearrange("p (g c) -> p g c", c=16)
            for t in range(2):
                self.tt("dve", tb0, bre, qb(11, t), ALU.mult, R, Wr)
                self.tt("dve", tb1, bim, qb(12, t), ALU.mult, R, Wr)
                self.tt("dve", bbar[:, t, 0], tb0, tb1, ALU.subtract, R, Wr)
                self.tt("dve", tb0, bim, qb(11, t), ALU.mult, R, Wr)
                self.tt("dve", tb1, bre, qb(12, t), ALU.mult, R, Wr)
                self.tt("dve", bbar[:, t, 1], tb0, tb1, ALU.add, R, Wr)
            VOFF = {"P1": (0.25, 0.5), "P2": (0.5, 0.75), "P5": (0.25, 0.0), "P6": (0.5, 0.25)}
            need = [(0, 0, ("P1", "P2")), (1, 0, ("P5", "P6")), (2, 0, ("P5", "P6")),
                    (1, 1, ("P1", "P2")), (2, 1, ("P1", "P2")), (0, 1, ("P5", "P6"))]
            pvi = {}
            n_pv = 0
            for (kset, t, vs) in need:
                kb = bass.AP(kvec, kset * 9, [[27, 128], [0, NG], [1, 9]])
                cyb = bass.AP(sc, 4 * 2 * NG + t * NG, [[16 * 2 * NG, 128], [1, NG], [0, 9]])
                lrb = bass.AP(sc, 1 * 2 * NG + t * NG, [[16 * 2 * NG, 128], [1, NG], [0, 9]])
                self.tt("dve", cyk[:], cyb, kb, ALU.mult, R, Wr)
                self.tt("dve", magk[:], lrb, kb, ALU.mult, R, Wr)
                self.act(magk[:], magk[:], AF.Exp, R, Wr)
                for v in vs:
                    lo, hi = VOFF[v]
                    self.ts("dve", cyo[0:64], cyk[0:64], lo, None, ALU.add, None, R, Wr)
                    self.ts("dve", cyo[64:128], cyk[64:128], hi, None, ALU.add, None, R, Wr)
                    self.copy("dve", cyi[:], cyo[:], R, Wr)
                    self.copy("dve", cyk2[:], cyi[:], R, Wr)
                    self.tt("dve", cyo[:], cyo[:], cyk2[:], ALU.subtract, R, Wr)
                    self.act(cyo[:], cyo[:], AF.Sin, R, Wr, scale=TWO_PI)
                    self.tt("dve", pv[:, n_pv], cyo[:], magk[:], ALU.mult, R, Wr)
                    pvi[(kset, t, v)] = n_pv
                    n_pv += 1

            def pvb(idx, g0):
                return bass.AP(pv, idx * NG * 9 + g0 * 9, [[12 * NG * 9, 128], [9, 8], [1, 9], [0, 16]])

            def xb(ap3, g0):
                base = ap3[:, g0:g0 + 8, :]
                return bass.AP(base.tensor, base.offset, [list(base.ap[0]), list(base.ap[1]), [0, 9], list(base.ap[2])])

            TABS = [
                ("LQf", cre, cim, 0, 0, "CP1"), ("VBf", cre, cim, 0, 0, "CP2"),
                ("RKf", bbar[:, 0, 0], bbar[:, 0, 1], 1, 0, "BP"), ("RK7f", bbar[:, 0, 0], bbar[:, 0, 1], 2, 0, "BP"),
                ("LQb", cre, cim, 1, 1, "CP1"), ("VAb", cre, cim, 2, 1, "CP1"), ("VBb", cre, cim, 2, 1, "CP2"),
                ("RKb", bbar[:, 1, 0], bbar[:, 1, 1], 0, 1, "BP"),
            ]
            TI = {n: i for i, (n, *_r) in enumerate(TABS)}
            btab, bT0, bW = self.B("s5tab"), [self.B("s5T0a"), self.B("s5T0b")], [self.B("s5Wa"), self.B("s5Wb")]
            bU8 = [[self.B(f"s5U{b}_{j}") for j in range(8)] for b in range(2)]
            bU = [tuple(bU8[0]), tuple(bU8[1])]
            btb = [self.B("s5tba"), self.B("s5tbb")]
            bsinb, bcosb = self.BL("s5sinb", 2), self.BL("s5cosb", 2)
            bG = [self.B("s5Ga"), self.B("s5Gb")]
            bH = [self.B("s5Ha"), self.B("s5Hb")]
            bHc = [self.BL("s5Hca", 2), self.BL("s5Hcb", 2)]
            bHs = [self.BL("s5Hsa", 2), self.BL("s5Hsb", 2)]
            byg = [self.B("s5yga"), self.B("s5ygb")]
            bt0 = self.B("s5t0sb")
            self.memset("pool", Hc[:], 0.0, tuple(bHc[0] + bHc[1]))
            self.memset("pool", Hs[:], 0.0, tuple(bHs[0] + bHs[1]))
            t4 = lambda a: a.rearrange("p g (k c) -> p g k c", c=16)
            btabs_t = [[self.B(f"s5tab{b}_{i}") for i in range(8)] for b in range(2)]
            btabs = [tuple(btabs_t[0]), tuple(btabs_t[1])]

            def build_tab(gb):
                g0 = gb * 8
                tbi = gb % 2
                for ti, (name, Xr, Xi, kset, t, kind) in enumerate(TABS):
                    o = t4(tab[:, tbi, ti])
                    a1, a2 = t4(tmp1[:]), t4(tmp2[:])
                    if kind == "CP1":
                        pa, pb_, op = pvi[(kset, t, "P1")], pvi[(kset, t, "P2")], ALU.add
                    elif kind == "CP2":
                        pa, pb_, op = pvi[(kset, t, "P2")], pvi[(kset, t, "P1")], ALU.subtract
                    else:
                        pa, pb_, op = pvi[(kset, t, "P5")], pvi[(kset, t, "P6")], ALU.add
                    if ti % 3 != 2:
                        en, a1, a2, b1, b2 = "dve", t4(tmp1[:]), t4(tmp2[:]), self.B("s5tmp1"), self.B("s5tmp2")
                    else:
                        en, a1, a2, b1, b2 = "pool", t4(tmp3[:]), t4(tmp4[:]), self.B("s5tmp3"), self.B("s5tmp4")
                    self.tt(en, a1, xb(Xr, g0), pvb(pa, g0), ALU.mult, R, (b1,))
                    self.tt(en, a2, xb(Xi, g0), pvb(pb_, g0), ALU.mult, R, (b2,))
                    self.tt(en, o, a1, a2, op, (b1, b2), (btabs_t[tbi][ti],))

            def tb(g, n, a, b):
                return tab[:, (g // 8) % 2, TI[n], g % 8, a * 16:b * 16]

            def front(g):
                bi = g % 2
                btab = btabs[(g // 8) % 2]
                for j in range(8):
                    T.dma("sp", U[j * 16:(j + 1) * 16, bi, :], ZC2[g * 16:(g + 1) * 16, j, :], reads=(), writes=(bU8[bi][j],))
                self.mm(ps[:, 7, 0:128], tb(g, "RKf", 0, 8), tb(g, "LQf", 0, 8), True, True, btab, (pb[7],), inc=False)
                self.mm(ps[:, 7, 128:256], tb(g, "RKb", 0, 8), tb(g, "LQb", 0, 8), True, True, btab, (pb[7],), inc=True)
                self.tt("dve", t0sb[:], ps[:, 7, 0:256], masks[:, 0:2, :].rearrange("p a b -> p (a b)"), ALU.mult,
                        (pb[7], bcst), (bt0,))
                self.tt("pool", t0s2[:], t0sb[:, 0:128], t0sb[:, 128:256], ALU.add, (bt0,), (bt0b,))
                self.ts("pool", t0s3[:], masks[:, 2, :], cdr[:, g:g + 1], None, ALU.mult, None, (bcst, bset), (bt0c,))
                self.tt("pool", T0[:, bi, :], t0s2[:], t0s3[:], ALU.add, (bt0b, bt0c), (bT0[bi],))
                self.mm(ps[:, 6, 0:128], tb(g, "RK7f", 1, 9), cmat[:, 0, :], True, True, btab + (bcst,), (pb[6],), inc=False)
                self.mm(ps[:, 6, 128:256], tb(g, "RK7f", 1, 9), cmat[:, 1, :], True, True, btab + (bcst,), (pb[6],), inc=False)
                self.mm(ps[:, 6, 256:384], tb(g, "RKb", 0, 8), cmat[:, 0, :], True, True, btab + (bcst,), (pb[6],), inc=False)
                self.mm(ps[:, 6, 384:512], tb(g, "RKb", 0, 8), cmat[:, 1, :], True, True, btab + (bcst,), (pb[6],), inc=True)
                self.copy("act", Wsb[:, bi, :], ps[:, 6, :], (pb[6],), (bW[bi],))
                cyb2 = bass.AP(cy8, g, [[2 * NG, 128], [NG, 2], [0, NCH]])
                iob2 = bass.AP(iota, 0, [[NCH, 128], [0, 2], [1, NCH]])
                self.tt("dve", yv[:], iob2, cyb2, ALU.mult, (bcst, bset), (byv,))
                self.copy("dve", yi[:], yv[:], (byv,), (byi,))
                self.tt("dve", fr[:], yv[:], yi[:], ALU.subtract, (byv, byi), (bfr,))
                self.act(nab[:], fr[:], AF.Abs, (bfr,), (bnab,))
                self.act(sinT[:, bi], fr[:], AF.Sin, (bfr,), (bsinb[bi],), scale=TWO_PI)
                self.act(cosT[:, bi], nab[:], AF.Sin, (bnab, cb), (bcosb[bi],), bias=self.epscol[:, 1:2], scale=-TWO_PI)

            def smm(g):
                bi = g % 2
                for d in range(2):
                    self.mm(ps[:, 2 * d, 0:NCH], Wsb[:, bi, d * 256:d * 256 + 128], U[:, bi, :], True, True,
                            (bW[bi],) + bU[bi], (pb[2 * d],))
                    self.mm(ps[:, 2 * d + 1, 0:NCH], Wsb[:, bi, d * 256 + 128:d * 256 + 256], U[:, bi, :], True, True,
                            (bW[bi],) + bU[bi], (pb[2 * d + 1],))

            def back(g):
                bi = g % 2
                btab = btabs[(g // 8) % 2]
                rev4 = lambda t4_, d: bass.AP(t4_, (bi * 2 + d) * NCH + NCH - 1, [[4 * NCH, 128], [-1, NCH]])
                rev3 = lambda t3, d: bass.AP(t3, (bi * 2 + d) * NCH + NCH - 1, [[4 * NCH, 128], [-1, NCH]])
                psr = lambda bank: bass.AP(ps, bank * 512 + NCH - 1, [[8 * 512, 128], [-1, NCH]])
                self.tt("dve", m1[:, 0, :], ps[:, 0, 0:NCH], cosT[:, bi, 0, :], ALU.mult, (pb[0], bcosb[bi]), (bm1[0],))
                self.tt("dve", m2[:, 0, :], ps[:, 1, 0:NCH], sinT[:, bi, 0, :], ALU.mult, (pb[1], bsinb[bi]), (bm2[0],))
                self.tt("dve", gp[:, 0, :], m1[:, 0, :], m2[:, 0, :], ALU.add, (bm1[0], bm2[0]), (bgp[0],))
                T.op("dve", lambda h: h.tensor_tensor_scan(out=gs[:, bi, 0, :], data0=r8[:, 0, g:g + 1].to_broadcast([128, NCH]),
                                                           data1=gp[:, 0, :], initial=0.0, op0=ALU.mult, op1=ALU.add),
                     (bgp[0], bset), (bGs[2 * bi],))
                self.tt("pool", Hc[:, bi, 0, 1:NCH + 1], gs[:, bi, 0, :], cosT[:, bi, 0, :], ALU.mult, (bGs[2 * bi], bcosb[bi]), (bHc[bi][0],))
                self.tt("pool", Hs[:, bi, 0, 1:NCH + 1], gs[:, bi, 0, :], sinT[:, bi, 0, :], ALU.mult, (bGs[2 * bi], bsinb[bi]), (bHs[bi][0],))
                self.tt("dve", m1[:, 1, :], psr(2), cosT[:, bi, 1, :], ALU.mult, (pb[2], bcosb[bi]), (bm1[1],))
                self.tt("dve", m2[:, 1, :], psr(3), sinT[:, bi, 1, :], ALU.mult, (pb[3], bsinb[bi]), (bm2[1],))
                self.tt("dve", gp[:, 1, :], m1[:, 1, :], m2[:, 1, :], ALU.add, (bm1[1], bm2[1]), (bgp[1],))
                T.op("dve", lambda h: h.tensor_tensor_scan(out=gs[:, bi, 1, :], data0=r8[:, 1, g:g + 1].to_broadcast([128, NCH]),
                                                           data1=gp[:, 1, :], initial=0.0, op0=ALU.mult, op1=ALU.add),
                     (bgp[1], bset), (bGs[2 * bi + 1],))
                self.tt("pool", Hc[:, bi, 1, 0:NCH], rev3(gs, 1), rev4(cosT, 1), ALU.mult, (bGs[2 * bi + 1], bcosb[bi]), (bHc[bi][1],))
                self.tt("pool", Hs[:, bi, 1, 0:NCH], rev3(gs, 1), rev4(sinT, 1), ALU.mult, (bGs[2 * bi + 1], bsinb[bi]), (bHs[bi][1],))

            def backB(g):
                bi = g % 2
                btab = btabs[(g // 8) % 2]
                bk = 4 + bi
                self.mm(ps[:, bk, 0:NCH], T0[:, bi, :], U[:, bi, :], True, False, (bT0[bi],) + bU[bi], (pb[bk],), inc=False)
                self.mm(ps[:, bk, 0:NCH], tb(g, "LQf", 1, 9), Hc[:, bi, 0, 0:NCH], False, False, btab + (bHc[bi][0],), (pb[bk],), inc=False)
                self.mm(ps[:, bk, 0:NCH], tb(g, "VBf", 1, 9), Hs[:, bi, 0, 0:NCH], False, False, btab + (bHs[bi][0],), (pb[bk],), inc=False)
                self.mm(ps[:, bk, 0:NCH], tb(g, "VAb", 0, 8), Hc[:, bi, 1, 1:NCH + 1], False, False, btab + (bHc[bi][1],), (pb[bk],), inc=False)
                self.mm(ps[:, bk, 0:NCH], tb(g, "VBb", 0, 8), Hs[:, bi, 1, 1:NCH + 1], False, True, btab + (bHs[bi][1],), (pb[bk],), inc=True)
                self.act(yg[:, bi, :], ps[:, bk, 0:NCH], AF.Gelu_apprx_tanh, (pb[bk],), (byg[bi],))
                for i in range(8):
                    T.dma("sp", YC2[g * 16:(g + 1) * 16, i, :], yg[i * 16:(i + 1) * 16, bi, :], reads=(byg[bi],), writes=())

            bGs = [self.B(f"s5gs{i}") for i in range(4)]
            build_tab(0)
            front(0)
            smm(0)
            for g in range(NG):
                back(g)
                if g + 1 < NG:
                    front(g + 1)
                    smm(g + 1)
                backB(g)
                if g % 8 == 3 and (g // 8 + 1) < NG // 8:
                    build_tab(g // 8 + 1)

    def phase2_glu(self, l, YC2, YT, W, vcol):
        nc, T, S, NCH = self.nc, self.T, self.S, self.NCH
        cb = self.B("consts")
        ps, pb = self.ps, self.pb
        with ExitStack() as st:
            ygT = self.sb(st, f"glyg{l}", [128, 2, 6, NCH], BF16)
            wg = self.sb(st, f"glw{l}", [128, 6, CW], BF16)
            sig = self.sb(st, f"glsig{l}", [128, 2, NCH], F32)
            o = self.sb(st, f"glo{l}", [128, 6, NCH], F32)
            osq = self.sb(st, f"glosq{l}", [128, 6, NCH], BF16)
            rstd = self.sb(st, f"glrstd{l}", [128, NCH], F32)
            ycs = self.sb(st, f"glycs{l}", [128, 6, S], BF16)
            bw = self.B("glw")
            scr, wb, _w, _k = W["glu"][0]
            while wb.w == ("pending",):
                self.issue_prep(1)
            T.dma("sp", wg[:], scr, reads=(wb,), writes=(bw,))
            byg = [self.B("glyg0"), self.B("glyg1")]
            bsig = [self.B("glsig0"), self.B("glsig1")]
            bo, bosq, brs, bycs = self.B("glo"), self.B("glosq"), self.B("glrstd"), self.B("glycs")
            bos_, bosqs, bycss = self.BL("glo", 6), self.BL("glosq", 6), self.BL("glycs", 6)
            for i in range(8):
                bi = i % 2
                self.issue_prep(2, after=(bos_[5],) if i > 0 else ())
                T.dma("sp", ygT[:, bi, :, :], YC2[:, i, :].rearrange("(fc p) n -> p fc n", p=128), reads=(), writes=(byg[bi],))
                for m in range(6):
                    bank = m % 4
                    for k in range(6):
                        self.mm(ps[:, bank, 0:NCH], wg[:, k, m * 128:(m + 1) * 128], ygT[:, bi, k, :], k == 0, k == 5,
                                (bw, byg[bi]), (pb[bank],))
                    sb_ = m % 2
                    self.act(sig[:, sb_, :], ps[:, bank, 0:NCH], AF.Sigmoid, (pb[bank], cb), (bsig[sb_],),
                             bias=vcol("bglu", l, m))
                    self.tt("dve", o[:, m, :], ygT[:, bi, m, :], sig[:, sb_, :], ALU.mult, (byg[bi], bsig[sb_]), (bos_[m],))
                    self.act(osq[:, m, :], o[:, m, :], AF.Square, (bos_[m],), (bosqs[m],))
                self.rstd_from_sq([osq[:, m, :] for m in range(6)], CW, 4 + bi, rstd[:], brs, tuple(bosqs))
                for m in range(6):
                    dst = bass.AP(ycs, m * S + i, [[6 * S, 128], [8, NCH]])
                    self.stt("dve", dst, o[:, m, :], vcol("og", l, 10 + m), rstd[:], ALU.mult, ALU.mult,
                             (bos_[m], brs, cb), (bycss[m],))
            T.dma("sp", YT[10:16, :, :].rearrange("k p t -> p k t"), ycs[:], reads=tuple(bycss), writes=())

    def phase2_fnet(self, l, PQ, YB, dftS_d, ogb_rep):
        nc, T, S, NC = self.nc, self.T, self.S, self.NC
        cb = self.B("consts")
        ps, pb = self.ps, self.pb
        with ExitStack() as st:
            pq = self.sb(st, f"fnpq{l}", [128, NC, 1024], BF16)
            dring = [self.sb(st, f"fnd{l}_{i}", [128, 2, NC, 128], BF16) for i in range(3)]
            ogb = self.sb(st, f"fnog{l}", [128, BW], F32)
            ysq = self.sb(st, f"fnysq{l}", [128, BW], F32)
            ss = self.sb(st, f"fnss{l}", [128, 2, 2], F32)
            yo = self.sb(st, f"fnyo{l}", [128, 2, BW], BF16)
            bpq, bog = self.B("fnpq"), self.B("fnog")
            bd = [self.B(f"fnd{i}") for i in range(3)]
            bd2 = [self.B(f"fnd2_{i}") for i in range(3)]
            bysq = self.B("fnysq")
            bss = [self.B("fnss0"), self.B("fnss1")]
            byo = [self.B("fnyo0"), self.B("fnyo1")]
            T.dma("sp", ogb[:], ogb_rep[l], (), (bog,))
            nq = max(1, NC // 8)
            for qd in range(0, NC, nq):
                T.dma("sp", pq[:, qd:qd + nq, :], PQ[qd * 128:(qd + nq) * 128, :].rearrange("(n p) f -> p n f", p=128),
                      reads=(), writes=(bpq,))
            for kb in range(NC):
                ri = kb % 3
                self.issue_prep(1, after=(byo[(kb + 1) % 2],) if kb > 1 else ())
                T.dma("sp", dring[ri][:, 0], dftS_d[0, kb], (), (bd[ri],))
                T.dma("sp", dring[ri][:, 1], dftS_d[1, kb], (), (bd2[ri],))
                bank = kb % 2
                n_mm = 2 * NC
                i_mm = 0
                for ch in range(NC):
                    for t in range(2):
                        self.mm(ps[:, bank, :], dring[ri][:, t, ch, :], pq[:, ch, t * 512:(t + 1) * 512], i_mm == 0,
                                i_mm == n_mm - 1, (bd[ri], bd2[ri], bpq), (pb[bank],))
                        i_mm += 1
                bi = kb % 2
                self.act(ysq[:], ps[:, bank, :], AF.Square, (pb[bank],), (bysq,))
                T.op("dve", lambda h, bi=bi: h.tensor_reduce(out=ss[:, bi, 0:1], in_=ysq[:], axis=AX.X, op=ALU.add),
                     (bysq,), (bss[bi],))
                self.act(ss[:, bi, 0:1], ss[:, bi, 0:1], AF.Sqrt, (bss[bi], cb), (bss[bi],), bias=self.epscol[:, 0:1],
                         scale=1.0 / BW)
                T.op("dve", lambda h, bi=bi: h.reciprocal(out=ss[:, bi, 0:1], in_=ss[:, bi, 0:1]), (bss[bi],), (bss[bi],))
                self.stt("dve", yo[:, bi, :], ps[:, bank, :], ss[:, bi, 0:1], ogb[:], ALU.mult, ALU.mult,
                         (pb[bank], bss[bi], bog), (byo[bi],))
                T.dma("act", YB[kb * 128:(kb + 1) * 128, :], yo[:, bi, :], reads=(byo[bi],), writes=())

    def phase3(self, l, XTl, XTn, YT, YB, W, ident, ident_bf, vcol, last, y_out):
        nc, T, S, NT = self.nc, self.T, self.S, self.NT
        cb = self.B("consts")
        ps, pb = self.ps, self.pb
        with ExitStack() as st:
            yT = self.sb(st, f"p3yT{l}", [128, KC, TT], BF16)
            ybt = self.sb(st, f"p3ybt{l}", [128, 4, BW], BF16)
            xr = self.sb(st, f"p3xr{l}", [128, KC, TT], F32)
            h2 = self.sb(st, f"p3h2{l}", [128, KC, TT], BF16)
            big = self.sb(st, f"p3big{l}", [128, FCH * TT], BF16)
            self.ring = [self.sb(st, f"p3ring{l}_{i}", [128, KC * 512], BF16) for i in range(4)]
            sl = self.sb(st, f"p3sl{l}", [128, 2, TT], F32)
            rstd = self.sb(st, f"p3rstd{l}", [128, TT], F32)
            actT = big[:].rearrange("p (f t) -> p f t", t=TT)
            osb_t = self.sb(st, f"p3os{l}", [128, 2, D], F32) if last else None
            sq = big[:, 0:KC * TT].rearrange("p (k t) -> p k t", t=TT)
            byT2 = self.B("p3yT2")
            byTm = self.BL("p3yTm", 4)
            bxrs = self.BL("p3xr", KC)
            bh2s = self.BL("p3h2", KC)
            bbigs = self.BL("p3big", FCH)
            yTbuf = lambda k: byT if k < 6 else (byTm[k - 6] if k < 10 else byT2)
            byT, bybt, bxr, bh2, bbig, brs = (self.B("p3yT"), self.B("p3ybt"), self.B("p3xr"), self.B("p3h2"),
                                              self.B("p3big"), self.B("p3rstd"))
            bsl = [self.B("p3sl0"), self.B("p3sl1")]
            bXT_in = self.B("XTa") if l % 2 == 0 else self.B("XTb")
            bXT_out = self.B("XTb") if l % 2 == 0 else self.B("XTa")
            for t in range(NT):
                t0 = t * TT
                if t > 0:
                    self.issue_prep(8, after=(bh2s[KC - 1],))
                def load_y(tt_):
                    q0 = tt_ * TT
                    T.dma("sp", yT[:, 0:6, :], YT[0:6, :, q0:q0 + TT].rearrange("k p t -> p k t"),
                          reads=(), writes=(byT,))
                    T.dma("sp", yT[:, 10:16, :], YT[10:16, :, q0:q0 + TT].rearrange("k p t -> p k t"),
                          reads=(), writes=(byT2,))
                    T.dma("sp", ybt[:], YB[q0:q0 + TT, :].rearrange("(n p) f -> p n f", p=128),
                          reads=(), writes=(bybt,))
                if t == 0:
                    load_y(0)
                out_views = [self.load_slab(W["out"][0]), self.load_slab(W["out"][1])]
                if last or t == 0:
                    T.dma("sp", xr[:], XTl[:, :, t0:t0 + TT].rearrange("k p t -> p k t"), reads=(), writes=tuple(bxrs))
                for h4 in range(4):
                    bank = 4 + h4 % 2
                    for n in range(4):
                        self.mm(ps[:, bank, n * 128:(n + 1) * 128], ybt[:, n, h4 * 128:(h4 + 1) * 128], ident_bf[:], True, True,
                                (bybt, cb), (pb[bank],), inc=(n == 3))
                    self.copy("act" if h4 % 2 == 0 else "dve", yT[:, 6 + h4, :], ps[:, bank, :], (pb[bank],), (byTm[h4],))
                for si, slab in enumerate(W["out"]):
                    wv, rb = out_views[si] if si < 2 else self.load_slab(slab)
                    for j in range(4):
                        m = si * 4 + j
                        bank = m % 4
                        for k in range(KC):
                            self.mm(ps[:, bank, :], wv[:, k, j * 128:(j + 1) * 128], yT[:, k, :], k == 0, k == KC - 1,
                                    (rb, yTbuf(k)), (pb[bank],))
                        self.tt("dve", xr[:, m, :], xr[:, m, :], ps[:, bank, :], ALU.add, (bxrs[m], pb[bank]), (bxrs[m],))
                        self.act(sq[:, m, :], xr[:, m, :], AF.Square, (bxrs[m],), (bbigs[m],))
                if t + 1 < NT:
                    load_y(t + 1)
                self.rstd_from_sq([sq[:, k, :] for k in range(KC)], D, 6, rstd[:], brs, tuple(bbigs[:KC]))
                for k in range(KC):
                    self.stt("dve", h2[:, k, :], xr[:, k, :], vcol("n2g", l, k), rstd[:],
                             ALU.mult, ALU.mult, (bxrs[k], brs, cb), (bh2s[k],))
                for si in range(11):
                    gv, grb = self.load_slab(W["gate"][si])
                    uv, urb = self.load_slab(W["up"][si])
                    for j in range(4):
                        f = si * 4 + j
                        bg, bu_ = (0, 1) if f % 2 == 0 else (2, 3)
                        for k in range(KC):
                            self.mm(ps[:, bg, :], gv[:, k, j * 128:(j + 1) * 128], h2[:, k, :], k == 0, k == KC - 1,
                                    (grb, bh2s[k]), (pb[bg],))
                        for k in range(KC):
                            self.mm(ps[:, bu_, :], uv[:, k, j * 128:(j + 1) * 128], h2[:, k, :], k == 0, k == KC - 1,
                                    (urb, bh2s[k]), (pb[bu_],))
                        sb_ = f % 2
                        self.act(sl[:, sb_, :], ps[:, bg, :], AF.Silu, (pb[bg],), (bsl[sb_],))
                        self.tt("dve", actT[:, f, :], sl[:, sb_, :], ps[:, bu_, :], ALU.mult, (bsl[sb_], pb[bu_]), (bbigs[f],))
                for m in range(KC):
                    wv, rb = self.load_slab(W["down"][m])
                    bank = 4 + m % 4
                    for f in range(FCH):
                        self.mm(ps[:, bank, :], wv[:, f, :], actT[:, f, :], f == 0, f == FCH - 1, (rb, bbigs[f]), (pb[bank],))
                    self.tt("dve", xr[:, m, :], xr[:, m, :], ps[:, bank, :], ALU.add, (bxrs[m], pb[bank]), (bxrs[m],))
                    if not last and m % 4 == 3:
                        c0_, c1_ = m - 3, m + 1
                        T.dma("act", XTn[c0_:c1_, :, t0:t0 + TT].rearrange("k p t -> p k t"), xr[:, c0_:c1_, :],
                              reads=tuple(bxrs[c0_:c1_]), writes=())
                        if t + 1 < NT:
                            T.dma("sp", xr[:, c0_:c1_, :], XTl[c0_:c1_, :, t0 + TT:t0 + 2 * TT].rearrange("k p t -> p k t"),
                                  reads=(), writes=tuple(bxrs[c0_:c1_]))
                if last:
                    for k in range(KC):
                        self.act(sq[:, k, :], xr[:, k, :], AF.Square, (bxrs[k],), (bbigs[k],))
                    self.rstd_from_sq([sq[:, k, :] for k in range(KC)], D, 6, rstd[:], brs, tuple(bbigs[:KC]))
                    for k in range(KC):
                        self.stt("dve", xr[:, k, :], xr[:, k, :], vcol("fg", 0, k), rstd[:],
                                 ALU.mult, ALU.mult, (bxrs[k], brs, cb), (bxrs[k],))
                    bos = [self.B("p3os0"), self.B("p3os1")]
                    osb = osb_t
                    for n in range(4):
                        ob = n % 2
                        for kg in range(4):
                            bank = (n * 4 + kg) % 4
                            for kk in range(4):
                                k = kg * 4 + kk
                                T.op("pe", lambda h, k=k, kk=kk, bank=bank, n=n: h.transpose(
                                    ps[:, bank, kk * 128:(kk + 1) * 128], xr[:, k, n * 128:(n + 1) * 128], ident[:]),
                                    reads=(bxrs[k], cb), writes=(pb[bank],), inc=(kk == 3))
                            self.copy("act" if kg % 2 == 0 else "dve", osb[:, ob, kg * 512:(kg + 1) * 512], ps[:, bank, :],
                                      (pb[bank],), (bos[ob],))
                        T.dma("act", y_out[t0 + n * 128:t0 + (n + 1) * 128, :], osb[:, ob, :], reads=(bos[ob],),
                              writes=())


_CACHE = {}


def _consts(S):
    NC = S // 128
    NCH = S // 8
    c = {}
    c["ident"] = np.eye(128, dtype=np.float32)
    J = np.zeros((128, 128), np.float32)
    for p in range(64):
        J[64 + p, p] = 1.0
        J[p, 64 + p] = -1.0
    c["cmat"] = np.stack([np.eye(128, dtype=np.float32), J], axis=1).astype(ml_dtypes.bfloat16)
    jj = np.arange(128) // 16
    mf = (jj[:, None] <= jj[None, :]).astype(np.float32)
    mb = (jj[:, None] >= jj[None, :]).astype(np.float32)
    c["masks"] = np.ascontiguousarray(np.stack([mf, mb, np.eye(128, dtype=np.float32)], axis=1))
    cc = np.arange(128)
    ang = 2.0 * np.pi * np.outer(cc, cc) / 128.0
    nrm = 1.0 / math.sqrt(S * 128.0)
    c["dft128"] = np.ascontiguousarray(np.stack([np.cos(ang) * nrm, -np.sin(ang) * nrm], axis=1)).astype(ml_dtypes.bfloat16)
    n = np.arange(S, dtype=np.int64)
    prod = np.outer(n, n) % S
    angS = (2.0 * np.pi / S) * prod
    tabs = []
    for f in (np.cos, np.sin):
        m = f(angS).astype(np.float32)
        m = m.reshape(NC, 128, NC, 128)
        tabs.append(np.ascontiguousarray(m.transpose(2, 1, 0, 3)))
    c["dftS"] = np.stack(tabs, axis=0).astype(ml_dtypes.bfloat16)
    kv = np.zeros((128, 3, 9), np.float32)
    kv[:, 0, :] = np.arange(9)
    kv[:, 1, :] = -np.arange(9)
    kv[:, 2, :] = 8 - np.arange(9)
    c["kvecs"] = kv
    c["iota"] = np.ascontiguousarray(np.broadcast_to(np.arange(NCH, dtype=np.float32), (128, NCH)))
    return c


def _pack_shared(inp):
    f = lambda a: np.ascontiguousarray(np.asarray(a, dtype=np.float32))
    d = {}
    d["w_in"] = f(inp["w_in"]); d["w_out"] = f(inp["w_out"]); d["w_gate"] = f(inp["w_gate"])
    d["w_up"] = f(inp["w_up"]); d["w_down"] = f(inp["w_down"]); d["w_glu"] = f(inp["c_w_glu"])
    d["ws_t"] = f(np.transpose(np.asarray(inp["a_ws"]), (0, 3, 1, 2)))
    vec = np.zeros((128, 124), np.float32)
    for l in range(NL):
        vec[:, 0 + l * 16:0 + (l + 1) * 16] = np.asarray(inp["norm1_g"])[l].reshape(16, 128).T
        vec[:, 32 + l * 16:32 + (l + 1) * 16] = np.asarray(inp["norm2_g"])[l].reshape(16, 128).T
        vec[:, 64 + l * 16:64 + (l + 1) * 16] = np.asarray(inp["out_norm_g"])[l].reshape(16, 128).T
        vec[:, 112 + l * 6:112 + (l + 1) * 6] = np.asarray(inp["c_b_glu"])[l].reshape(6, 128).T
    vec[:, 96:112] = np.asarray(inp["final_g"]).reshape(16, 128).T
    d["vecs"] = vec
    rep = lambda a: np.ascontiguousarray(np.broadcast_to(np.asarray(a, np.float32)[:, None, :], (NL, 128, a.shape[-1])))
    d["vg_rep"] = rep(np.asarray(inp["a_v_g"]))
    d["bs_rep"] = rep(np.asarray(inp["a_bs"]).reshape(NL, AW))
    d["ogb_rep"] = rep(np.asarray(inp["out_norm_g"])[:, AW:AW + BW])
    lam = np.stack([np.asarray(inp["c_lam_re"]), np.asarray(inp["c_lam_im"])], axis=1)
    lam = np.transpose(lam, (0, 4, 1, 2, 3))
    d["lam"] = f(np.concatenate([lam, lam], axis=1))
    d["lstep"] = f(np.broadcast_to(np.asarray(inp["c_log_step"])[:, None], (NL, 128, 2, NG)))
    bre = np.transpose(np.asarray(inp["c_b_re"]), (0, 2, 1, 3))
    bim = np.transpose(np.asarray(inp["c_b_im"]), (0, 2, 1, 3))
    cre = np.transpose(np.asarray(inp["c_c_re"]), (0, 3, 1, 2))
    cim = np.transpose(np.asarray(inp["c_c_im"]), (0, 3, 1, 2))
    bc = np.stack([bre, bim, cre, cim], axis=2)
    d["bc"] = f(np.concatenate([bc, bc], axis=1))
    cd = np.asarray(inp["c_d"]).reshape(NL, NG, 16)
    cd = np.transpose(cd, (0, 2, 1))
    d["cd_rep"] = f(np.tile(cd, (1, 8, 1)))
    return d


def build_program(S):
    b = Builder(S)
    return b.build()


def run(seqs, inp, n_cores=8, trace=False):
    S = seqs[0].shape[0]
    key = ("nc", S)
    if key not in _CACHE:
        _CACHE[key] = build_program(S)
    nc = _CACHE[key]
    shared = _pack_shared(inp)
    shared.update(_consts(S))
    if n_cores == 8 and len(seqs) == 6:
        core_of_seq = [0, 1, 2, 4, 5, 6]
    else:
        core_of_seq = list(range(len(seqs)))
    seq_of_core = {c: i for i, c in enumerate(core_of_seq)}
    zero_x = None
    in_maps = []
    for c in range(n_cores):
        m = dict(shared)
        if c in seq_of_core:
            m["x"] = np.ascontiguousarray(seqs[seq_of_core[c]], dtype=np.float32)
        else:
            if zero_x is None:
                zero_x = np.zeros((S, D), np.float32)
            m["x"] = zero_x
        in_maps.append(m)
    res = run_bass_kernel_spmd(nc, in_maps, core_ids=list(range(n_cores)), trace=trace)
    outs = [np.asarray(res.results[core_of_seq[i]]["y"], dtype=np.float32) for i in range(len(seqs))]
    return outs, res


def kernel(**inputs):
    xp = np.asarray(inputs["x_prompt"], dtype=np.float32)
    xs = np.asarray(inputs["x_sample"], dtype=np.float32)
    seqs = [xp[i] for i in range(xp.shape[0])] + [xs[i] for i in range(xs.shape[0])]
    outs, _ = run(seqs, inputs, n_cores=8)
    yp = np.stack(outs[:xp.shape[0]], axis=0)
    ys = np.stack(outs[xp.shape[0]:], axis=0)
    return (yp, ys)
```
